# Optimizing a Trainium2 kernel written in Bass

```python
import math
import jax, jax.numpy as jnp
from jax import lax
import numpy as np

D_MODEL = 1024
BATCH = 8
SEQ = 4096
DEPTH = 2

D_MIX = D_MODEL
D_A = D_MIX // 2
D_B = D_MIX - D_A
GROUP_DIM = 64
K_A = 3
K_B = 31
N_MEM = 256
N_XHEADS = 4
XHEAD_DIM = D_MODEL // N_XHEADS
D_FF = 4 * D_MODEL
D_IN = 3 * D_A + 2 * D_B
EPS = 1e-6

kernel_name = "hybrid_shortconv_conformer_memxattn"


def rms_norm(x, g):
    x32 = x.astype(jnp.float32)
    y = x32 * lax.rsqrt(jnp.mean(x32 * x32, axis=-1, keepdims=True) + EPS)
    return (y * g.astype(jnp.float32)).astype(x.dtype)


def layer_norm(x, g, b):
    x32 = x.astype(jnp.float32)
    mu = jnp.mean(x32, axis=-1, keepdims=True)
    var = jnp.mean(jnp.square(x32 - mu), axis=-1, keepdims=True)
    y = (x32 - mu) * lax.rsqrt(var + EPS)
    return (y * g.astype(jnp.float32) + b.astype(jnp.float32)).astype(x.dtype)


def causal_dwconv(x, w):
    k, c = w.shape
    return lax.conv_general_dilated(
        x, w.reshape(k, 1, c).astype(x.dtype),
        window_strides=(1,), padding=[(k - 1, 0)],
        dimension_numbers=("NWC", "WIO", "NWC"),
        feature_group_count=c)


def mixer_block(u, w_in, conv_a_w, conv_b_w, conv_b_bias, ln_b_g, ln_b_b, w_out):
    z = jnp.einsum("bsd,de->bse", u, w_in)
    b_a, c_a, h_a, v_b, g_b = jnp.split(
        z, [D_A, 2 * D_A, 3 * D_A, 3 * D_A + D_B], axis=-1)
    y_a = b_a * causal_dwconv(c_a * h_a, conv_a_w)
    glu = v_b * jax.nn.sigmoid(g_b)
    cb = causal_dwconv(glu, conv_b_w) + conv_b_bias.astype(glu.dtype)
    y_b = jax.nn.silu(layer_norm(cb, ln_b_g, ln_b_b))
    y = jnp.concatenate([y_a, y_b], axis=-1)
    return jnp.einsum("bse,ed->bsd", y, w_out)


def memory_cross_attention(q_in, m, w_q, w_kv, w_xo):
    b, s, _ = q_in.shape
    q = jnp.einsum("bsd,de->bse", q_in, w_q).reshape(b, s, N_XHEADS, XHEAD_DIM)
    kv = jnp.einsum("bmd,de->bme", m, w_kv)
    k, v = jnp.split(kv, 2, axis=-1)
    k = k.reshape(b, N_MEM, N_XHEADS, XHEAD_DIM)
    v = v.reshape(b, N_MEM, N_XHEADS, XHEAD_DIM)
    scores = jnp.einsum("bshd,bmhd->bhsm", q, k).astype(jnp.float32) * (1.0 / math.sqrt(XHEAD_DIM))
    p = jax.nn.softmax(scores, axis=-1).astype(v.dtype)
    o = jnp.einsum("bhsm,bmhd->bshd", p, v).reshape(b, s, D_MODEL)
    return jnp.einsum("bse,ed->bsd", o, w_xo)


def sqrelu_mlp(u, w_up, w_down):
    h = jnp.square(jax.nn.relu(jnp.einsum("bsd,df->bsf", u, w_up)))
    return jnp.einsum("bsf,fd->bsd", h, w_down)


def setup_inputs(seed: int = 0) -> dict:
    key = jax.random.key(seed)
    ks = jax.random.split(key, 20)
    f32 = jnp.float32

    def nrm(k, shape, scale):
        return jax.random.normal(k, shape, f32) * scale

    def gain(k, shape):
        return 1.0 + 0.02 * jax.random.normal(k, shape, f32)

    res_scale = (2.0 * DEPTH) ** -0.5
    return {
        "x": nrm(ks[0], (BATCH, SEQ, D_MODEL), 1.0),
        "mem": nrm(ks[1], (BATCH, N_MEM, D_MODEL), 1.0),
        "norm_mix_g": gain(ks[2], (DEPTH, D_MODEL)),
        "w_in": nrm(ks[3], (DEPTH, D_MODEL, D_IN), D_MODEL ** -0.5),
        "conv_a_w": nrm(ks[4], (DEPTH, K_A, D_A), K_A ** -0.5),
        "conv_b_w": nrm(ks[5], (DEPTH, K_B, D_B), K_B ** -0.5),
        "conv_b_bias": nrm(ks[6], (DEPTH, D_B), 0.02),
        "ln_b_g": gain(ks[7], (DEPTH, D_B)),
        "ln_b_b": nrm(ks[8], (DEPTH, D_B), 0.02),
        "w_out": nrm(ks[9], (DEPTH, D_MIX, D_MODEL), D_MIX ** -0.5 * res_scale),
        "norm_x_g": gain(ks[10], (DEPTH, D_MODEL)),
        "norm_mem_g": gain(ks[11], (DEPTH, D_MODEL)),
        "w_q": nrm(ks[12], (DEPTH, D_MODEL, D_MODEL), D_MODEL ** -0.5),
        "w_kv": nrm(ks[13], (DEPTH, D_MODEL, 2 * D_MODEL), D_MODEL ** -0.5),
        "w_xo": nrm(ks[14], (DEPTH, D_MODEL, D_MODEL), D_MODEL ** -0.5 * res_scale),
        "norm_ffn_g": gain(ks[15], (DEPTH, D_MODEL)),
        "w_up": nrm(ks[16], (DEPTH, D_MODEL, D_FF), D_MODEL ** -0.5),
        "w_down": nrm(ks[17], (DEPTH, D_FF, D_MODEL), D_FF ** -0.5 * res_scale),
        "final_g": gain(ks[18], (D_MODEL,)),
    }


def reference(x, mem, norm_mix_g, w_in, conv_a_w, conv_b_w, conv_b_bias, ln_b_g, ln_b_b, w_out,
              norm_x_g, norm_mem_g, w_q, w_kv, w_xo, norm_ffn_g, w_up, w_down, final_g):
    h = x
    for l in range(DEPTH):
        u = rms_norm(h, norm_mix_g[l])
        h = h + mixer_block(u, w_in[l], conv_a_w[l], conv_b_w[l], conv_b_bias[l],
                            ln_b_g[l], ln_b_b[l], w_out[l])
        q_in = rms_norm(h, norm_x_g[l])
        m = rms_norm(mem, norm_mem_g[l])
        h = h + memory_cross_attention(q_in, m, w_q[l], w_kv[l], w_xo[l])
        h = h + sqrelu_mlp(rms_norm(h, norm_ffn_g[l]), w_up[l], w_down[l])
    return rms_norm(h, final_g)
```

```python
from contextlib import ExitStack

import numpy as np
import concourse.bass as bass
import concourse.mybir as mybir
from concourse.bass_utils import run_bass_kernel_spmd

F32 = mybir.dt.float32
BF16 = mybir.dt.bfloat16
ALU = mybir.AluOpType
AF = mybir.ActivationFunctionType

D = 1024
SEQ = 4096
NB = 8
NMEM = 256
DEPTH = 2
DFF = 4096
T = 512
EPS = 1e-6
KB = 31
KA = 3
HB = KB - 1
HA = KA - 1
NCH_L = 35
NW_L = 31
C_D = 31
RING = 6
NWST = 2
NDT = 4
NV_L = 180
NV = 2 * NV_L + 8 + 1
GRAN = {"sb": 512, "ps": 2048, "wbf": 1}

C_KV = 0
C_IN = 4
C_OUT = 9
C_Q = 11
C_XO = 13
C_UP = 15
C_DN = 23
V_MIX, V_X, V_FFN, V_MEM, V_CA, V_CB, V_CBB, V_LNG, V_LNB = 0, 8, 16, 24, 32, 44, 168, 172, 176
V_FINAL = 2 * NV_L
V_EPS = 2 * NV_L + 8


class Ref:
    __slots__ = ("ap", "space", "lo", "hi")

    def __init__(self, ap, space, lo, hi):
        self.ap, self.space, self.lo, self.hi = ap, space, lo, hi

    def grans(self):
        g = GRAN[self.space]
        return [(self.space, i) for i in range(self.lo // g, (self.hi - 1) // g + 1)]


class Buf:
    def __init__(self, arena, off, nelem, dt, stride=T):
        self.off, self.n, self.dt, self.stride = off, nelem, dt, stride
        self.esz = 4 if dt == F32 else 2
        assert off % 4 == 0 and (nelem * self.esz) % 4 == 0
        w = arena[:, off // 4:(off + nelem * self.esz) // 4]
        self.full = w if dt == F32 else w.bitcast(BF16)

    def r(self, lo=0, hi=None):
        hi = self.n if hi is None else hi
        return Ref(self.full[:, lo:hi], "sb", self.off + lo * self.esz, self.off + hi * self.esz)

    def c(self, j, a=0, b=None):
        b = self.stride if b is None else b
        return self.r(j * self.stride + a, j * self.stride + b)

    def r3(self, j0, j1, a=0, b=None):
        b = self.stride if b is None else b
        ap = self.full[:, j0 * self.stride:j1 * self.stride].rearrange("p (c t) -> p c t", t=self.stride)[:, :, a:b]
        return Ref(ap, "sb", self.off + (j0 * self.stride + a) * self.esz, self.off + ((j1 - 1) * self.stride + b) * self.esz)


class Sched:
    ENGS = ("pe", "act", "dve", "pool", "sp")

    def __init__(self, esem):
        self.esem = esem
        self.streams = {e: [] for e in self.ENGS}
        self.count = {}
        self.known = {e: {} for e in self.ENGS}
        self.g = {}
        self.semh = {}
        self.psi = 0

    def op(self, eng, emit, reads=(), writes=(), dma_sem=None):
        needs = {}

        def need(tok):
            if tok is None:
                return
            sem, val, who = tok
            if needs.get(sem.name, (None, 0))[1] < val:
                needs[sem.name] = (sem, val)

        is_dma = dma_sem is not None
        for r in reads:
            for g in r.grans():
                st = self.g.get(g)
                if st is not None and st[0] is not None:
                    need(st[0])
                if st is not None and r.space == "ps":
                    for tok in st[1].values():
                        if tok[2] != eng:
                            need(tok)
        for w in writes:
            for g in w.grans():
                st = self.g.get(g)
                if st is None:
                    continue
                if st[0] is not None and (is_dma or st[0][2] != eng):
                    need(st[0])
                for tok in st[1].values():
                    if is_dma or tok[2] != eng:
                        need(tok)
        if is_dma:
            sem = dma_sem
            amt = 16
            who = "dma"
            prev = self.count.get(sem.name, 0)
            if prev > 0:
                need((sem, prev, "dma"))
        else:
            sem = self.esem[eng]
            amt = 1
            who = eng
        self.semh[sem.name] = sem
        self.count[sem.name] = self.count.get(sem.name, 0) + amt
        tok = (sem, self.count[sem.name], who)
        waits = []
        kn = self.known[eng]
        for name, (s, v) in needs.items():
            if kn.get(name, 0) < v:
                kn[name] = v
                waits.append((s, v))
        self.streams[eng].append((waits, emit, sem, amt))
        for w in writes:
            for g in w.grans():
                self.g[g] = [tok, {}]
        for r in reads:
            for g in r.grans():
                st = self.g.get(g)
                if st is None:
                    st = [None, {}]
                    self.g[g] = st
                st[1][sem.name] = tok
        return tok

    def wait_all(self, eng, sems):
        waits = [(s, self.count.get(s.name, 0)) for s in sems if self.count.get(s.name, 0) > 0]
        self.streams[eng].append((waits, None, None, 0))

    def emit(self, eng, e):
        for waits, emit, sem, amt in self.streams[eng]:
            for s, v in waits:
                e.wait_ge(s, v)
            if emit is not None:
                ins = emit(e)
                ins.then_inc(sem, amt)


def build_nc(ntiles=SEQ // T, nlayers=DEPTH):
    nc = bass.Bass("TRN2", target_bir_lowering=False)
    xT = nc.dram_tensor("xT", [D, SEQ], F32, kind="ExternalInput").ap()
    memT = nc.dram_tensor("memT", [D, NMEM], F32, kind="ExternalInput").ap()
    wts = nc.dram_tensor("wts", [DEPTH * NW_L, 128, 4096], F32, kind="ExternalInput").ap()
    vecs = nc.dram_tensor("vecs", [128, NV], F32, kind="ExternalInput").ap()
    ident_d = nc.dram_tensor("ident", [128, 128], F32, kind="ExternalInput").ap()
    outT = nc.dram_tensor("outT", [D, SEQ], F32, kind="ExternalOutput").ap()
    wbf = nc.dram_tensor("wbf", [DEPTH * NCH_L, 128, 4096], BF16, kind="Internal").ap()
    xT3 = xT.rearrange("(c p) t -> p c t", p=128)
    outT3 = outT.rearrange("(c p) t -> p c t", p=128)
    memT3 = memT.rearrange("(c p) t -> p c t", p=128)

    es = ExitStack()
    with es:
        ARENA_BYTES = 206 * 1024
        arena_t = es.enter_context(nc.sbuf_tensor("arena", [128, ARENA_BYTES // 4], F32))
        arena = arena_t[:, :]
        psums = [es.enter_context(nc.psum_tensor(f"ps{i}", [128, 512], F32)) for i in range(8)]
        sem = {}
        for n in ("pe", "act", "dve", "pool", "spq", "x0", "x1", "ost", "misc") + tuple(f"w{i}" for i in range(RING)) + tuple(f"wc{i}" for i in range(RING)) + tuple(f"wst{i}" for i in range(NWST)):
            sem[n] = es.enter_context(nc.semaphore(n))
        S = Sched({"pe": sem["pe"], "act": sem["act"], "dve": sem["dve"], "pool": sem["pool"], "sp": sem["spq"]})

        off = [0]

        def alloc(nelem, dt, stride=T):
            esz = 4 if dt == F32 else 2
            nb = (nelem * esz + 511) // 512 * 512
            o = off[0]
            off[0] += nb
            assert o + nb <= ARENA_BYTES, (o, nb)
            return Buf(arena, o, nelem, dt, stride)

        slots = [alloc(4096, BF16) for _ in range(RING)]
        KT = [alloc(8 * NMEM, BF16, NMEM) for _ in range(DEPTH)]
        VV = [alloc(2 * D, BF16, D) for _ in range(DEPTH)]
        vec = alloc(NV, F32, 1)
        ones = alloc(128, BF16, 128)
        identb = alloc(128, F32, 128)
        gluh = [alloc(4 * HB, BF16, HB) for _ in range(DEPTH)]
        chh = [alloc(4 * HA, F32, HA) for _ in range(DEPTH)]
        hbuf = [alloc(8 * T, F32), alloc(8 * T, F32)]
        xn = alloc(8 * T, BF16)
        sq = alloc(8 * T, BF16)
        rstd = alloc(T, F32)
        rstd_n = alloc(T, F32)
        sd = alloc(T, F32)
        xg = alloc(8 * T, BF16)
        dummy = alloc(128, F32, 128)
        ostage = alloc(8 * T, F32)
        U0 = off[0]
        v_b = alloc(4 * T, F32)
        sig = alloc(4 * T, F32)
        glub = alloc(4 * (T + HB), BF16, T + HB)
        cb = alloc(4 * T, F32)
        c_a = alloc(4 * T, F32)
        chb = alloc(4 * (T + HA), F32, T + HA)
        ycat = alloc(8 * T, BF16)
        mean = alloc(T, F32)
        msq = alloc(T, F32)
        U1 = off[0]
        off[0] = U0
        qT = alloc(8 * T, BF16)
        PT = [alloc(2 * T, BF16) for _ in range(4)]
        rinv = [alloc(T, F32) for _ in range(4)]
        lnt = [alloc(T, F32) for _ in range(2)]
        oT = alloc(8 * T, BF16)
        assert off[0] <= U1
        off[0] = U0
        hid = alloc(32 * T, BF16)
        rtmp = [alloc(T, F32) for _ in range(4)]
        assert off[0] <= U1
        off[0] = U0
        memf = alloc(8 * NMEM, F32, NMEM)
        dstage = [alloc(4096, BF16), alloc(4096, BF16)]
        dstage1 = [Buf(arena, ostage.off, 4096, BF16), Buf(arena, ostage.off + 8192, 4096, BF16)]
        assert off[0] <= U1
        off[0] = U1

        eps_ref = vec.r(V_EPS, V_EPS + 1)
        vref = vec.r()

        def next_ps(n=T):
            i = S.psi % 6
            S.psi += 1
            return Ref(psums[i][:, 0:n], "ps", i * 2048, i * 2048 + 2048)

        aux = [0]

        def next_aux(n=T):
            i = 6 + aux[0] % 2
            aux[0] += 1
            return Ref(psums[i][:, 0:n], "ps", i * 2048, i * 2048 + 2048)

        mix_order = [C_IN + 0, C_IN + 2, C_IN + 1, C_IN + 3, C_D, C_D + 1, C_D + 2, C_D + 3, C_IN + 4]
        seq = []
        for l in range(nlayers):
            seq += [l * NCH_L + C_KV + i for i in range(4)]
        for ti in range(ntiles):
            for l in range(nlayers):
                seq += [l * NCH_L + i for i in mix_order]
                seq += [l * NCH_L + i for i in range(C_OUT, C_D)]
        ring = {"pos": 0, "loaded": 0, "seen": set()}
        wst_i = [0]

        def next_wst():
            i = wst_i[0] % NWST
            wst_i[0] += 1
            return sem[f"wst{i}"]

        def ring_load(i):
            cid = seq[i]
            slot = slots[i % RING]
            wsem = sem[f"w{i % RING}"]
            if cid not in ring["seen"]:
                ring["seen"].add(cid)
                l, ci = divmod(cid, NCH_L)
                assert ci < NW_L
                S.op("pool", lambda e, o=slot.full, s=wts[l * NW_L + ci]: e.dma_start(out=o, in_=s),
                     writes=[slot.r()], dma_sem=sem[f"wc{i % RING}"])
                if ntiles > 1 and ci >= C_IN:
                    S.op("sp", lambda e, o=wbf[cid], s=slot.full: e.dma_start(out=o, in_=s),
                         reads=[slot.r()], writes=[Ref(None, "wbf", cid, cid + 1)], dma_sem=next_wst())
            else:
                S.op("sp", lambda e, o=slot.full, s=wbf[cid]: e.dma_start(out=o, in_=s),
                     reads=[Ref(None, "wbf", cid, cid + 1)], writes=[slot.r()], dma_sem=wsem)

        def ring_get(cid):
            p = ring["pos"]
            assert seq[p] == cid, (p, seq[p], cid)
            while ring["loaded"] < min(len(seq), p + RING):
                ring_load(ring["loaded"])
                ring["loaded"] += 1
            ring["pos"] = p + 1
            return slots[p % RING]

        def mm(ps, pairs, reads):
            def emit(e, ps=ps, pairs=pairs):
                n = len(pairs)
                ins = None
                for i, (l, r) in enumerate(pairs):
                    ins = e.matmul(ps.ap, lhsT=l, rhs=r, start=(i == 0), stop=(i == n - 1))
                return ins
            S.op("pe", emit, reads=reads, writes=[ps])

        def mm1(ps, lhsT, rhs, start, stop, reads):
            S.op("pe", lambda e: e.matmul(ps.ap, lhsT=lhsT, rhs=rhs, start=start, stop=stop), reads=reads, writes=[ps])

        def act(out, in_, func, scale=1.0, bias=None, extra_reads=()):
            def emit(e):
                if bias is None:
                    return e.activation(out=out.ap, in_=in_.ap, func=func, scale=scale)
                return e.activation(out=out.ap, in_=in_.ap, func=func, scale=scale, bias=bias)
            S.op("act", emit, reads=[in_] + list(extra_reads), writes=[out])

        def tt(eng, out, a, b, op):
            S.op(eng, lambda e: e.tensor_tensor(out=out.ap, in0=a.ap, in1=b.ap, op=op), reads=[a, b], writes=[out])

        def ts(eng, out, a, s1, s2, op0, op1=None, extra_reads=()):
            def emit(e):
                if op1 is None:
                    return e.tensor_scalar(out=out.ap, in0=a.ap, scalar1=s1, scalar2=None, op0=op0)
                return e.tensor_scalar(out=out.ap, in0=a.ap, scalar1=s1, scalar2=s2, op0=op0, op1=op1)
            S.op(eng, emit, reads=[a] + list(extra_reads), writes=[out])

        def stt(eng, out, a, s, b, op0, op1, extra_reads=()):
            S.op(eng, lambda e: e.scalar_tensor_tensor(out=out.ap, in0=a.ap, scalar=s, in1=b.ap, op0=op0, op1=op1),
                 reads=[a, b] + list(extra_reads), writes=[out])

        def copy(eng, out, a):
            S.op(eng, lambda e: e.tensor_copy(out=out.ap, in_=a.ap), reads=[a], writes=[out])

        def memset(eng, out, val):
            S.op(eng, lambda e: e.memset(out.ap, val), writes=[out])

        def W3(slot, kind):
            if kind == "A":
                return slot.full.rearrange("p (k c) -> p k c", c=512)
            return slot.full.rearrange("p (k c) -> p k c", c=128)

        def rstd_from(ps, scale, nt=T, out=None):
            out = rstd if out is None else out
            act(sd.r(0, nt), ps, AF.Ln, scale=scale, bias=eps_ref.ap, extra_reads=[eps_ref])
            act(out.r(0, nt), sd.r(0, nt), AF.Exp, scale=-0.5)

        def apply_norm(src, gcol, dst, nt=T, rs=None, engs=("dve",)):
            rs = rstd if rs is None else rs
            for c in range(8):
                eng = engs[c % len(engs)]
                d_, s_ = dst.r(c * nt, (c + 1) * nt), src.r(c * nt, (c + 1) * nt)
                if eng == "pool":
                    assert dst.dt == F32
                    tt("pool", d_, s_, rs.r(0, nt), ALU.mult)
                    ts("pool", d_, d_, vec.full[:, gcol + c:gcol + c + 1], None, ALU.mult, extra_reads=[vref])
                else:
                    stt(eng, d_, s_, vec.full[:, gcol + c:gcol + c + 1], rs.r(0, nt), ALU.mult, ALU.mult, extra_reads=[vref])

        tcnt = [0]

        def touch(func):
            tcnt[0] += 1
            c = 1 + tcnt[0] % 127
            S.op("act", lambda e, c=c: e.activation(out=dummy.full[:, c:c + 1], in_=dummy.full[:, 0:1], func=func),
                 reads=[dummy.r(0, 1)], writes=[])

        def stats_oneshot(src, nt):
            act(sq.r(0, 8 * nt), src.r(0, 8 * nt), AF.Square)
            ps = next_aux(nt)
            mm(ps, [(ones.full, sq.full[:, c * nt:(c + 1) * nt]) for c in range(8)], [ones.r(), sq.r(0, 8 * nt)])
            return ps

        def proj_group(slot, j, src, ps, nk=8, kind="A"):
            w3 = W3(slot, kind)
            if kind == "A":
                pairs = [(w3[:, k, j * 128:(j + 1) * 128], src.full[:, k * T:(k + 1) * T]) for k in range(nk)]
                mm(ps, pairs, [slot.r(), src.r(0, nk * T)])
            else:
                for q in range(4):
                    ks = range(q * 8, q * 8 + 8)

                    def emit(e, ks=ks):
                        ins = None
                        for k in ks:
                            ins = e.matmul(ps.ap, lhsT=w3[:, k, :], rhs=src.full[:, k * T:(k + 1) * T],
                                           start=(k == 0), stop=(k == nk - 1))
                        return ins
                    S.op("pe", emit, reads=[slot.r(), src.r(q * 8 * T, (q + 1) * 8 * T)], writes=[ps])

        def chunk_kouter(slot, src, nk=8, hook=None, hook_at=4):
            pss = [next_ps() for _ in range(4)]
            w3 = W3(slot, "A")
            for k in range(nk):
                def emit(e, k=k):
                    ins = None
                    for j in range(4):
                        ins = e.matmul(pss[j].ap, lhsT=w3[:, k, j * 128:(j + 1) * 128], rhs=src.full[:, k * T:(k + 1) * T],
                                       start=(k == 0), stop=(k == nk - 1))
                    return ins
                S.op("pe", emit, reads=[slot.r(), src.c(k)], writes=pss)
                if hook is not None and k == hook_at:
                    hook()
            return pss

        def resid_phase(cids, kind, src, h, gnext=None):
            st = next_aux()
            cnt = [0]
            pend = []

            def stat(m):
                first, last = cnt[0] == 0, cnt[0] == 7
                cnt[0] += 1
                mm1(st, ones.full, sq.full[:, m * T:(m + 1) * T], first, last, [ones.r(), sq.c(m)])

            m = 0
            for ci, cid in enumerate(cids):
                slot = ring_get(cid)
                ng = 4 if kind == "A" else 1
                pss = chunk_kouter(slot, src) if (kind == "A" and ci == 0) else None
                for j in range(ng):
                    if pss is None:
                        ps = next_ps()
                        proj_group(slot, j, src, ps, nk=(8 if kind == "A" else 32), kind=kind)
                    else:
                        ps = pss[j]
                    tt("dve", h.c(m), h.c(m), ps, ALU.add)
                    act(sq.c(m), h.c(m), AF.Square)
                    if gnext is not None:
                        act(xg.c(m), h.c(m), AF.Copy, scale=vec.full[:, gnext + m:gnext + m + 1], extra_reads=[vref])
                    pend.append(m)
                    m += 1
                    if len(pend) > (4 if kind == "A" else 2):
                        stat(pend.pop(0))

            def flush():
                for mm_ in pend:
                    stat(mm_)
                assert cnt[0] == 8
            return st, flush

        def boundary(st, flush, h, gcol, first_cid, need_xn=True):
            slot = ring_get(first_cid)
            pss = chunk_kouter(slot, xg, hook=flush, hook_at=4)
            rstd_from(st, 1.0 / D)
            if need_xn:
                apply_norm(h, gcol, xn)
            return pss

        S.op("sp", lambda e: e.dma_start(out=vec.full, in_=vecs), writes=[vec.r()], dma_sem=sem["misc"])
        S.op("sp", lambda e: e.dma_start(out=identb.full, in_=ident_d), writes=[identb.r()], dma_sem=sem["misc"])
        S.op("sp", lambda e: e.dma_start(out=memf.r3(0, 8).ap, in_=memT3), writes=[memf.r()], dma_sem=sem["misc"])
        S.op("sp", lambda e: e.dma_start(out=hbuf[0].r3(0, 8).ap, in_=xT3[:, :, 0:T]), writes=[hbuf[0].r()], dma_sem=sem["x0"])
        memset("dve", ones.r(), 1.0)
        for l in range(DEPTH):
            memset("dve", gluh[l].r(), 0.0)
            memset("dve", chh[l].r(), 0.0)
        def build_diag(l, mode, js=range(4)):
            for j in js:
                n = l * 4 + j
                stage = (dstage if mode == "act" else dstage1)[n % 2]
                wc = l * NV_L + V_CB + j * KB
                if mode == "act":
                    for k in range(KB):
                        S.op("act", lambda e, o=stage.full[:, k * 128:(k + 1) * 128], sc=vec.full[:, wc + k:wc + k + 1]:
                             e.activation(out=o, in_=identb.full, func=AF.Copy, scale=sc),
                             reads=[identb.r(), vref], writes=[stage.r(k * 128, (k + 1) * 128)])
                else:
                    st3 = stage.full[:, 0:KB * 128].rearrange("p (k q) -> p k q", q=128)
                    id3 = identb.full.unsqueeze(1).to_broadcast([128, KB, 128])
                    w3_ = vec.full[:, wc:wc + KB].unsqueeze(2).to_broadcast([128, KB, 128])
                    S.op("dve", lambda e, o=st3, a=id3, b=w3_: e.tensor_tensor(out=o, in0=a, in1=b, op=ALU.mult),
                         reads=[identb.r(), vref], writes=[stage.r()])
                cid = l * NCH_L + C_D + j
                S.op("sp", lambda e, o=wbf[cid], s=stage.full: e.dma_start(out=o, in_=s),
                     reads=[stage.r()], writes=[Ref(None, "wbf", cid, cid + 1)], dma_sem=next_wst())
                ring["seen"].add(cid)

        for stg in dstage + dstage1:
            memset("dve", stg.r(KB * 128, 4096), 0.0)
        build_diag(0, "dve", [2, 3])
        stm = stats_oneshot(memf, NMEM)
        rstd_from(stm, 1.0 / D, NMEM)
        build_diag(0, "act", [0, 1])
        for l in range(nlayers):
            vb = l * NV_L
            for c in range(8):
                stt("dve", xn.r(c * NMEM, (c + 1) * NMEM), memf.c(c), vec.full[:, vb + V_MEM + c:vb + V_MEM + c + 1],
                    rstd.r(0, NMEM), ALU.mult, ALU.mult, extra_reads=[vref])
            for cc in range(2):
                slot = ring_get(l * NCH_L + C_KV + cc)
                w3 = W3(slot, "A")
                for j in range(4):
                    ps = next_ps(NMEM)
                    mm(ps, [(w3[:, k, j * 128:(j + 1) * 128], xn.full[:, k * NMEM:(k + 1) * NMEM]) for k in range(8)],
                       [slot.r(), xn.r(0, 8 * NMEM)])
                    copy("dve", KT[l].c(cc * 4 + j), ps)
            for cc in range(2):
                slot = ring_get(l * NCH_L + C_KV + 2 + cc)
                w3 = W3(slot, "A")
                for mc in range(2):
                    ps = next_ps()
                    mm(ps, [(xn.full[:, k * NMEM + mc * 128:k * NMEM + (mc + 1) * 128], w3[:, k, :]) for k in range(8)],
                       [slot.r(), xn.r(0, 8 * NMEM)])
                    copy("dve", VV[l].c(mc, cc * 512, (cc + 1) * 512), ps)

        memset("dve", dummy.r(), 1.0)
        final_pending = None

        def emit_final(fp):
            st_, flush_, h_, t0_ = fp
            flush_()
            rstd_from(st_, 1.0 / D)
            apply_norm(h_, V_FINAL, ostage)
            S.op("sp", lambda e, o=outT3[:, :, t0_:t0_ + T], s=ostage.r3(0, 8).ap: e.dma_start(out=o, in_=s),
                 reads=[ostage.r()], dma_sem=sem["ost"])

        for ti in range(ntiles):
            h = hbuf[ti % 2]
            t0 = ti * T
            if ti == 0:
                st0 = stats_oneshot(h, T)
                rstd_from(st0, 1.0 / D, out=rstd_n)
                apply_norm(h, V_MIX, xn, rs=rstd_n)
            pss_first = None
            for l in range(nlayers):
                vb = l * NV_L
                base = l * NCH_L
                last_layer = l == nlayers - 1
                if pss_first is None:
                    slot = ring_get(base + C_IN + 0)
                    hook = None
                    if final_pending is not None:
                        fp = final_pending
                        final_pending = None
                        hook = lambda fp=fp: emit_final(fp)
                    pss = chunk_kouter(slot, xn, hook=hook, hook_at=4)
                    scaled = False
                else:
                    pss = pss_first
                    scaled = True
                touch(AF.Sigmoid)
                if l == 0 and ti + 1 < ntiles:
                    nh = hbuf[(ti + 1) % 2]
                    S.op("sp", lambda e, o=nh.r3(0, 8).ap, s=xT3[:, :, t0 + T:t0 + 2 * T]: e.dma_start(out=o, in_=s),
                         writes=[nh.r()], dma_sem=sem[f"x{(ti + 1) % 2}"])
                copy("dve", glub.r3(0, 4, 0, HB), gluh[l].r3(0, 4))
                copy("dve", chb.r3(0, 4, 0, HA), chh[l].r3(0, 4))
                for j in range(4):
                    if scaled:
                        tt("dve", v_b.c(j), pss[j], rstd.r(), ALU.mult)
                    else:
                        act(v_b.c(j), pss[j], AF.Copy)
                if scaled:
                    apply_norm(h, vb + V_MIX, xn)
                slot = ring_get(base + C_IN + 2)
                for j in range(4):
                    ps = next_ps()
                    proj_group(slot, j, xg if scaled else xn, ps)
                    if scaled:
                        tt("dve", c_a.c(j), ps, rstd.r(), ALU.mult)
                    else:
                        act(c_a.c(j), ps, AF.Copy)
                slot = ring_get(base + C_IN + 1)
                for j in range(4):
                    ps = next_ps()
                    proj_group(slot, j, xn, ps)
                    act(sig.c(j), ps, AF.Sigmoid)
                    tt("dve", glub.c(j, HB, HB + T), v_b.c(j), sig.c(j), ALU.mult)
                touch(AF.Ln)
                copy("dve", gluh[l].r3(0, 4), glub.r3(0, 4, T, T + HB))
                wcb = vb + V_CB

                def partial(j):
                    if NDT == 0:
                        return
                    ts("dve", cb.c(j), glub.c(j, 0, T), vec.full[:, wcb + j * KB:wcb + j * KB + 1],
                       vec.full[:, vb + V_CBB + j:vb + V_CBB + j + 1], ALU.mult, ALU.add, extra_reads=[vref])
                    for k in range(1, NDT):
                        stt("dve", cb.c(j), glub.c(j, k, k + T), vec.full[:, wcb + j * KB + k:wcb + j * KB + k + 1],
                            cb.c(j), ALU.mult, ALU.add, extra_reads=[vref])

                wa = vb + V_CA

                def conv_a(j):
                    ts("dve", c_a.c(j), chb.c(j, 0, T), vec.full[:, wa + j:wa + j + 1], None, ALU.mult, extra_reads=[vref])
                    for k in range(1, KA):
                        stt("dve", c_a.c(j), chb.c(j, k, k + T), vec.full[:, wa + k * 4 + j:wa + k * 4 + j + 1],
                            c_a.c(j), ALU.mult, ALU.add, extra_reads=[vref])

                partial(0)
                slot = ring_get(base + C_IN + 3)
                for j in range(4):
                    ps = next_ps()
                    proj_group(slot, j, xn, ps)
                    tt("dve", chb.c(j, HA, HA + T), ps, c_a.c(j), ALU.mult)
                copy("dve", chh[l].r3(0, 4), chb.r3(0, 4, T, T + HA))
                partial(1)
                conv_a(0)
                partial(2)
                conv_a(1)
                partial(3)
                conv_a(2)
                conv_a(3)
                ps1 = next_aux()
                ps2 = next_aux()
                pend = []

                def ln_stat(j):
                    mm1(ps1, ones.full, sq.full[:, j * T:(j + 1) * T], j == 0, j == 3, [ones.r(), sq.c(j)])
                    mm1(ps2, ones.full, sq.full[:, (4 + j) * T:(5 + j) * T], j == 0, j == 3, [ones.r(), sq.c(4 + j)])

                for j in range(4):
                    slot = ring_get(base + C_D + j)
                    psc = next_ps()
                    g0 = j * (T + HB)
                    mm(psc, [(slot.full[:, k * 128:(k + 1) * 128], glub.full[:, g0 + k:g0 + k + T]) for k in range(NDT, KB)],
                       [slot.r(), glub.c(j, 0, T + HB)])
                    if NDT == 0:
                        act(cb.c(j), psc, AF.Identity, bias=vec.full[:, vb + V_CBB + j:vb + V_CBB + j + 1], extra_reads=[vref])
                    else:
                        tt("dve", cb.c(j), psc, cb.c(j), ALU.add)
                    copy("dve", sq.c(j), cb.c(j))
                    act(sq.c(4 + j), cb.c(j), AF.Square)
                    pend.append(j)
                    if NDT == 0 and len(pend) > 1:
                        ln_stat(pend.pop(0))
                slot = ring_get(base + C_IN + 4)
                psb = []
                for j in range(4):
                    ps = next_ps()
                    proj_group(slot, j, xn, ps)
                    psb.append(ps)
                    if NDT == 0:
                        if j == 1:
                            for jj in pend:
                                ln_stat(jj)
                    elif j < 2:
                        ln_stat(pend[2 * j])
                        ln_stat(pend[2 * j + 1])

                def ya(j):
                    tt("dve", ycat.c(j), psb[j], c_a.c(j), ALU.mult)

                def ln_a(j):
                    tt("dve", cb.c(j), cb.c(j), mean.r(), ALU.subtract)

                def ln_b(j):
                    tt("dve", cb.c(j), cb.c(j), rstd.r(), ALU.mult)
                    act(ycat.c(4 + j), cb.c(j), AF.Silu, scale=vec.full[:, vb + V_LNG + j:vb + V_LNG + j + 1],
                        bias=vec.full[:, vb + V_LNB + j:vb + V_LNB + j + 1], extra_reads=[vref])

                ya(0)
                ts("dve", mean.r(), ps1, 1.0 / 512, None, ALU.mult)
                tt("dve", msq.r(), mean.r(), mean.r(), ALU.mult)
                stt("dve", msq.r(), ps2, 1.0 / 512, msq.r(), ALU.mult, ALU.subtract)
                ts("dve", msq.r(), msq.r(), 0.0, None, ALU.max)
                rstd_from(msq.r(), 1.0)
                touch(AF.Silu)
                ln_a(0)
                ya(1)
                ln_a(1)
                ln_b(0)
                ya(2)
                ln_b(1)
                ln_a(2)
                ln_b(2)
                ya(3)
                ln_a(3)
                ln_b(3)
                touch(AF.Ln)
                st, flush = resid_phase([base + C_OUT, base + C_OUT + 1], "A", ycat, h, gnext=vb + V_X)

                pss = boundary(st, flush, h, vb + V_X, base + C_Q, need_xn=False)
                for j in range(4):
                    stt("dve", qT.c(j), pss[j], 1.0 / 16.0, rstd.r(), ALU.mult, ALU.mult)
                slot = ring_get(base + C_Q + 1)
                for j in range(4):
                    ps = next_ps()
                    proj_group(slot, j, xg, ps)
                    stt("dve", qT.c(4 + j), ps, 1.0 / 16.0, rstd.r(), ALU.mult, ALU.mult)

                def scores(hd):
                    for mc in range(2):
                        ps = next_ps()
                        mm(ps, [(KT[l].full[:, (hd * 2 + dk) * NMEM + mc * 128:(hd * 2 + dk) * NMEM + (mc + 1) * 128],
                                 qT.full[:, (hd * 2 + dk) * T:(hd * 2 + dk + 1) * T]) for dk in range(2)],
                           [KT[l].r(), qT.c(hd * 2), qT.c(hd * 2 + 1)])
                        act(PT[hd].c(mc), ps, AF.Exp)

                def sums_pv(hd):
                    pss_ = next_aux()
                    mm(pss_, [(ones.full, PT[hd].full[:, mc * T:(mc + 1) * T]) for mc in range(2)], [ones.r(), PT[hd].r()])
                    act(lnt[hd % 2].r(), pss_, AF.Ln)
                    act(rinv[hd].r(), lnt[hd % 2].r(), AF.Exp, scale=-1.0)
                    for dc in range(2):
                        ps = next_ps()
                        mm(ps, [(VV[l].full[:, mc * D + hd * 256 + dc * 128:mc * D + hd * 256 + (dc + 1) * 128],
                                 PT[hd].full[:, mc * T:(mc + 1) * T]) for mc in range(2)],
                           [VV[l].r(), PT[hd].r()])
                        tt("dve", oT.c(hd * 2 + dc), ps, rinv[hd].r(), ALU.mult)

                scores(0)
                scores(1)
                sums_pv(0)
                scores(2)
                sums_pv(1)
                scores(3)
                sums_pv(2)
                sums_pv(3)
                st, flush = resid_phase([base + C_XO, base + C_XO + 1], "A", oT, h, gnext=vb + V_FFN)

                pss = boundary(st, flush, h, vb + V_FFN, base + C_UP)
                pre = last_layer and ti + 1 < ntiles
                hn = hbuf[(ti + 1) % 2]
                stn = None
                gi = 0
                for cc in range(8):
                    slot = ring_get(base + C_UP + cc) if cc > 0 else None
                    if ti == 0 and l == 0 and nlayers > 1 and cc % 2 == 1:
                        build_diag(1, "dve", [cc // 2])
                    for j in range(4):
                        rt = rtmp[gi % 4]
                        if cc == 0:
                            ps = pss[j]
                        else:
                            ps = next_ps()
                            proj_group(slot, j, xg if cc == 1 else xn, ps)
                        act(rt.r(), ps, AF.Relu)
                        if cc <= 1:
                            tt("dve", rt.r(), rt.r(), rstd.r(), ALU.mult)
                        tt("dve", hid.c(cc * 4 + j), rt.r(), rt.r(), ALU.mult)
                        if pre:
                            if 8 <= gi < 16:
                                act(sq.c(gi - 8), hn.c(gi - 8), AF.Square)
                            if 10 <= gi < 18:
                                if stn is None:
                                    stn = next_aux()
                                c = gi - 10
                                mm1(stn, ones.full, sq.full[:, c * T:(c + 1) * T], c == 0, c == 7, [ones.r(), sq.c(c)])
                            if gi == 18:
                                rstd_from(stn, 1.0 / D, out=rstd_n)
                        gi += 1
                gnext = None if last_layer else (l + 1) * NV_L + V_MIX
                st, flush = resid_phase([base + C_DN + m for m in range(8)], "B", hid, h, gnext=gnext)
                if not last_layer:
                    pss_first = boundary(st, flush, h, gnext, (l + 1) * NCH_L + C_IN + 0, need_xn=False)
                else:
                    if ti + 1 < ntiles:
                        apply_norm(hn, V_MIX, xn, rs=rstd_n)
                    final_pending = (st, flush, h, t0)
        emit_final(final_pending)

        S.wait_all("sp", [sem["ost"]] + [sem[f"wst{i}"] for i in range(NWST)])

        with nc.Block() as block:
            @block.tensor
            def _(e):
                S.emit("pe", e)

            @block.scalar
            def _(e):
                S.emit("act", e)

            @block.vector
            def _(e):
                S.emit("dve", e)

            @block.gpsimd
            def _(e):
                S.emit("pool", e)

            @block.sync
            def _(e):
                S.emit("sp", e)
    return nc


def _kind_a(W, c0):
    return np.ascontiguousarray(W[:, c0:c0 + 512].reshape(8, 128, 512).transpose(1, 0, 2)).reshape(128, 4096)


def _kind_b(W, c0):
    return np.ascontiguousarray(W[:, c0:c0 + 128].reshape(32, 128, 128).transpose(1, 0, 2)).reshape(128, 4096)


def _pack(inp):
    wts = np.empty((DEPTH * NW_L, 128, 4096), np.float32)
    vecs = np.zeros((128, NV), np.float32)

    def fm(v):
        v = np.asarray(v, np.float32)
        return v.reshape(-1, 128).T

    for l in range(DEPTH):
        b = l * NW_L
        wkv = np.asarray(inp["w_kv"][l])
        for i in range(4):
            wts[b + C_KV + i] = _kind_a(wkv, i * 512)
        win = np.asarray(inp["w_in"][l])
        for i, c0 in enumerate((1536, 2048, 512, 1024, 0)):
            wts[b + C_IN + i] = _kind_a(win, c0)
        for name, cb_, n in (("w_out", C_OUT, 2), ("w_q", C_Q, 2), ("w_xo", C_XO, 2), ("w_up", C_UP, 8)):
            w = np.asarray(inp[name][l])
            for i in range(n):
                wts[b + cb_ + i] = _kind_a(w, i * 512)
        wd = np.asarray(inp["w_down"][l])
        for i in range(8):
            wts[b + C_DN + i] = _kind_b(wd, i * 128)
        v = l * NV_L
        vecs[:, v + V_MIX:v + V_MIX + 8] = fm(inp["norm_mix_g"][l])
        vecs[:, v + V_X:v + V_X + 8] = fm(inp["norm_x_g"][l])
        vecs[:, v + V_FFN:v + V_FFN + 8] = fm(inp["norm_ffn_g"][l])
        vecs[:, v + V_MEM:v + V_MEM + 8] = fm(inp["norm_mem_g"][l])
        for k in range(KA):
            vecs[:, v + V_CA + k * 4:v + V_CA + k * 4 + 4] = fm(inp["conv_a_w"][l][k])
        for k in range(KB):
            wk = fm(inp["conv_b_w"][l][k])
            for j in range(4):
                vecs[:, v + V_CB + j * KB + k] = wk[:, j]
        vecs[:, v + V_CBB:v + V_CBB + 4] = fm(inp["conv_b_bias"][l])
        vecs[:, v + V_LNG:v + V_LNG + 4] = fm(inp["ln_b_g"][l])
        vecs[:, v + V_LNB:v + V_LNB + 4] = fm(inp["ln_b_b"][l])
    vecs[:, V_FINAL:V_FINAL + 8] = fm(inp["final_g"])
    vecs[:, V_EPS] = EPS
    return wts, vecs


def run(inp, ntiles=SEQ // T, nlayers=DEPTH, cores=NB, trace=False):
    wts, vecs = _pack(inp)
    x = np.asarray(inp["x"], np.float32)
    mem = np.asarray(inp["mem"], np.float32)
    nc = build_nc(ntiles, nlayers)
    ident = np.eye(128, dtype=np.float32)
    in_maps = []
    for b in range(cores):
        in_maps.append({"xT": np.ascontiguousarray(x[b].T), "memT": np.ascontiguousarray(mem[b].T),
                        "wts": wts, "vecs": vecs, "ident": ident})
    res = run_bass_kernel_spmd(nc, in_maps, core_ids=list(range(cores)), trace=trace)
    out = np.stack([np.ascontiguousarray(res.results[b]["outT"].T) for b in range(cores)], axis=0)
    return out, res


def kernel(**inputs):
    out, _ = run(inputs)
    return out.astype(np.float32)
```

```python
from contextlib import ExitStack

import numpy as np
import concourse.bass as bass
import concourse.mybir as mybir
from concourse.bass_utils import run_bass_kernel_spmd

F32 = mybir.dt.float32
BF16 = mybir.dt.bfloat16
ALU = mybir.AluOpType
AF = mybir.ActivationFunctionType

D = 1024
SEQ = 4096
NB = 8
NMEM = 256
DEPTH = 2
DFF = 4096
T = 512
EPS = 1e-6
KB = 31
KA = 3
HB = KB - 1
HA = KA - 1
NCH_L = 35
NW_L = 31
C_D = 31
RING = 6
NWST = 2
NDT = 4
NV_L = 180
NV = 2 * NV_L + 8 + 1
GRAN = {"sb": 512, "ps": 2048, "wbf": 1}

C_KV = 0
C_IN = 4
C_OUT = 9
C_Q = 11
C_XO = 13
C_UP = 15
C_DN = 23
V_MIX, V_X, V_FFN, V_MEM, V_CA, V_CB, V_CBB, V_LNG, V_LNB = 0, 8, 16, 24, 32, 44, 168, 172, 176
V_FINAL = 2 * NV_L
V_EPS = 2 * NV_L + 8


class Ref:
    __slots__ = ("ap", "space", "lo", "hi")

    def __init__(self, ap, space, lo, hi):
        self.ap, self.space, self.lo, self.hi = ap, space, lo, hi

    def grans(self):
        g = GRAN[self.space]
        return [(self.space, i) for i in range(self.lo // g, (self.hi - 1) // g + 1)]


class Buf:
    def __init__(self, arena, off, nelem, dt, stride=T):
        self.off, self.n, self.dt, self.stride = off, nelem, dt, stride
        self.esz = 4 if dt == F32 else 2
        assert off % 4 == 0 and (nelem * self.esz) % 4 == 0
        w = arena[:, off // 4:(off + nelem * self.esz) // 4]
        self.full = w if dt == F32 else w.bitcast(BF16)

    def r(self, lo=0, hi=None):
        hi = self.n if hi is None else hi
        return Ref(self.full[:, lo:hi], "sb", self.off + lo * self.esz, self.off + hi * self.esz)

    def c(self, j, a=0, b=None):
        b = self.stride if b is None else b
        return self.r(j * self.stride + a, j * self.stride + b)

    def r3(self, j0, j1, a=0, b=None):
        b = self.stride if b is None else b
        ap = self.full[:, j0 * self.stride:j1 * self.stride].rearrange("p (c t) -> p c t", t=self.stride)[:, :, a:b]
        return Ref(ap, "sb", self.off + (j0 * self.stride + a) * self.esz, self.off + ((j1 - 1) * self.stride + b) * self.esz)


class Sched:
    ENGS = ("pe", "act", "dve", "pool", "sp")

    def __init__(self, esem):
        self.esem = esem
        self.streams = {e: [] for e in self.ENGS}
        self.count = {}
        self.known = {e: {} for e in self.ENGS}
        self.g = {}
        self.semh = {}
        self.psi = 0

    def op(self, eng, emit, reads=(), writes=(), dma_sem=None):
        needs = {}

        def need(tok):
            if tok is None:
                return
            sem, val, who = tok
            if needs.get(sem.name, (None, 0))[1] < val:
                needs[sem.name] = (sem, val)

        is_dma = dma_sem is not None
        for r in reads:
            for g in r.grans():
                st = self.g.get(g)
                if st is not None and st[0] is not None:
                    need(st[0])
                if st is not None and r.space == "ps":
                    for tok in st[1].values():
                        if tok[2] != eng:
                            need(tok)
        for w in writes:
            for g in w.grans():
                st = self.g.get(g)
                if st is None:
                    continue
                if st[0] is not None and (is_dma or st[0][2] != eng):
                    need(st[0])
                for tok in st[1].values():
                    if is_dma or tok[2] != eng:
                        need(tok)
        if is_dma:
            sem = dma_sem
            amt = 16
            who = "dma"
            prev = self.count.get(sem.name, 0)
            if prev > 0:
                need((sem, prev, "dma"))
        else:
            sem = self.esem[eng]
            amt = 1
            who = eng
        self.semh[sem.name] = sem
        self.count[sem.name] = self.count.get(sem.name, 0) + amt
        tok = (sem, self.count[sem.name], who)
        waits = []
        kn = self.known[eng]
        for name, (s, v) in needs.items():
            if kn.get(name, 0) < v:
                kn[name] = v
                waits.append((s, v))
        self.streams[eng].append((waits, emit, sem, amt))
        for w in writes:
            for g in w.grans():
                self.g[g] = [tok, {}]
        for r in reads:
            for g in r.grans():
                st = self.g.get(g)
                if st is None:
                    st = [None, {}]
                    self.g[g] = st
                st[1][sem.name] = tok
        return tok

    def wait_all(self, eng, sems):
        waits = [(s, self.count.get(s.name, 0)) for s in sems if self.count.get(s.name, 0) > 0]
        self.streams[eng].append((waits, None, None, 0))

    def emit(self, eng, e):
        for waits, emit, sem, amt in self.streams[eng]:
            for s, v in waits:
                e.wait_ge(s, v)
            if emit is not None:
                ins = emit(e)
                ins.then_inc(sem, amt)


def build_nc(ntiles=SEQ // T, nlayers=DEPTH):
    nc = bass.Bass("TRN2", target_bir_lowering=False)
    xT = nc.dram_tensor("xT", [D, SEQ], F32, kind="ExternalInput").ap()
    memT = nc.dram_tensor("memT", [D, NMEM], F32, kind="ExternalInput").ap()
    wts = nc.dram_tensor("wts", [DEPTH * NW_L, 128, 4096], F32, kind="ExternalInput").ap()
    vecs = nc.dram_tensor("vecs", [128, NV], F32, kind="ExternalInput").ap()
    ident_d = nc.dram_tensor("ident", [128, 128], F32, kind="ExternalInput").ap()
    outT = nc.dram_tensor("outT", [D, SEQ], F32, kind="ExternalOutput").ap()
    wbf = nc.dram_tensor("wbf", [DEPTH * NCH_L, 128, 4096], BF16, kind="Internal").ap()
    xT3 = xT.rearrange("(c p) t -> p c t", p=128)
    outT3 = outT.rearrange("(c p) t -> p c t", p=128)
    memT3 = memT.rearrange("(c p) t -> p c t", p=128)

    es = ExitStack()
    with es:
        ARENA_BYTES = 206 * 1024
        arena_t = es.enter_context(nc.sbuf_tensor("arena", [128, ARENA_BYTES // 4], F32))
        arena = arena_t[:, :]
        psums = [es.enter_context(nc.psum_tensor(f"ps{i}", [128, 512], F32)) for i in range(8)]
        sem = {}
        for n in ("pe", "act", "dve", "pool", "spq", "x0", "x1", "ost", "misc") + tuple(f"w{i}" for i in range(RING)) + tuple(f"wc{i}" for i in range(RING)) + tuple(f"wst{i}" for i in range(NWST)):
            sem[n] = es.enter_context(nc.semaphore(n))
        S = Sched({"pe": sem["pe"], "act": sem["act"], "dve": sem["dve"], "pool": sem["pool"], "sp": sem["spq"]})

        off = [0]

        def alloc(nelem, dt, stride=T):
            esz = 4 if dt == F32 else 2
            nb = (nelem * esz + 511) // 512 * 512
            o = off[0]
            off[0] += nb
            assert o + nb <= ARENA_BYTES, (o, nb)
            return Buf(arena, o, nelem, dt, stride)

        slots = [alloc(4096, BF16) for _ in range(RING)]
        KT = [alloc(8 * NMEM, BF16, NMEM) for _ in range(DEPTH)]
        VV = [alloc(2 * D, BF16, D) for _ in range(DEPTH)]
        vec = alloc(NV, F32, 1)
        ones = alloc(128, BF16, 128)
        identb = alloc(128, F32, 128)
        gluh = [alloc(4 * HB, BF16, HB) for _ in range(DEPTH)]
        chh = [alloc(4 * HA, F32, HA) for _ in range(DEPTH)]
        hbuf = [alloc(8 * T, F32), alloc(8 * T, F32)]
        xn = alloc(8 * T, BF16)
        sq = alloc(8 * T, BF16)
        rstd = alloc(T, F32)
        rstd_n = alloc(T, F32)
        sd = alloc(T, F32)
        xg = alloc(8 * T, BF16)
        dummy = alloc(128, F32, 128)
        ostage = alloc(8 * T, F32)
        U0 = off[0]
        v_b = alloc(4 * T, F32)
        sig = alloc(4 * T, F32)
        glub = alloc(4 * (T + HB), BF16, T + HB)
        cb = alloc(4 * T, F32)
        c_a = alloc(4 * T, F32)
        chb = alloc(4 * (T + HA), F32, T + HA)
        ycat = alloc(8 * T, BF16)
        mean = alloc(T, F32)
        msq = alloc(T, F32)
        U1 = off[0]
        off[0] = U0
        qT = alloc(8 * T, BF16)
        PT = [alloc(2 * T, BF16) for _ in range(4)]
        rinv = [alloc(T, F32) for _ in range(4)]
        lnt = [alloc(T, F32) for _ in range(2)]
        oT = alloc(8 * T, BF16)
        assert off[0] <= U1
        off[0] = U0
        hid = alloc(32 * T, BF16)
        rtmp = [alloc(T, F32) for _ in range(4)]
        assert off[0] <= U1
        off[0] = U0
        memf = alloc(8 * NMEM, F32, NMEM)
        dstage = [alloc(4096, BF16), alloc(4096, BF16)]
        dstage1 = [Buf(arena, ostage.off, 4096, BF16), Buf(arena, ostage.off + 8192, 4096, BF16)]
        assert off[0] <= U1
        off[0] = U1

        eps_ref = vec.r(V_EPS, V_EPS + 1)
        vref = vec.r()

        def next_ps(n=T):
            i = S.psi % 6
            S.psi += 1
            return Ref(psums[i][:, 0:n], "ps", i * 2048, i * 2048 + 2048)

        aux = [0]

        def next_aux(n=T):
            i = 6 + aux[0] % 2
            aux[0] += 1
            return Ref(psums[i][:, 0:n], "ps", i * 2048, i * 2048 + 2048)

        mix_order = [C_IN + 0, C_IN + 2, C_IN + 1, C_IN + 3, C_D, C_D + 1, C_D + 2, C_D + 3, C_IN + 4]
        seq = []
        for l in range(nlayers):
            seq += [l * NCH_L + C_KV + i for i in range(4)]
        for ti in range(ntiles):
            for l in range(nlayers):
                seq += [l * NCH_L + i for i in mix_order]
                seq += [l * NCH_L + i for i in range(C_OUT, C_D)]
        ring = {"pos": 0, "loaded": 0, "seen": set()}
        wst_i = [0]

        def next_wst():
            i = wst_i[0] % NWST
            wst_i[0] += 1
            return sem[f"wst{i}"]

        def ring_load(i):
            cid = seq[i]
            slot = slots[i % RING]
            wsem = sem[f"w{i % RING}"]
            if cid not in ring["seen"]:
                ring["seen"].add(cid)
                l, ci = divmod(cid, NCH_L)
                assert ci < NW_L
                S.op("pool", lambda e, o=slot.full, s=wts[l * NW_L + ci]: e.dma_start(out=o, in_=s),
                     reads=([memf.r()] if i < RING else []), writes=[slot.r()], dma_sem=sem[f"wc{i % RING}"])
                if ntiles > 1 and ci >= C_IN:
                    S.op("sp", lambda e, o=wbf[cid], s=slot.full: e.dma_start(out=o, in_=s),
                         reads=[slot.r()], writes=[Ref(None, "wbf", cid, cid + 1)], dma_sem=next_wst())
            else:
                S.op("sp", lambda e, o=slot.full, s=wbf[cid]: e.dma_start(out=o, in_=s),
                     reads=[Ref(None, "wbf", cid, cid + 1)], writes=[slot.r()], dma_sem=wsem)

        def ring_get(cid):
            p = ring["pos"]
            assert seq[p] == cid, (p, seq[p], cid)
            while ring["loaded"] < min(len(seq), p + RING):
                ring_load(ring["loaded"])
                ring["loaded"] += 1
            ring["pos"] = p + 1
            return slots[p % RING]

        def mm(ps, pairs, reads):
            def emit(e, ps=ps, pairs=pairs):
                n = len(pairs)
                ins = None
                for i, (l, r) in enumerate(pairs):
                    ins = e.matmul(ps.ap, lhsT=l, rhs=r, start=(i == 0), stop=(i == n - 1))
                return ins
            S.op("pe", emit, reads=reads, writes=[ps])

        def mm1(ps, lhsT, rhs, start, stop, reads):
            S.op("pe", lambda e: e.matmul(ps.ap, lhsT=lhsT, rhs=rhs, start=start, stop=stop), reads=reads, writes=[ps])

        def act(out, in_, func, scale=1.0, bias=None, extra_reads=()):
            def emit(e):
                if bias is None:
                    return e.activation(out=out.ap, in_=in_.ap, func=func, scale=scale)
                return e.activation(out=out.ap, in_=in_.ap, func=func, scale=scale, bias=bias)
            S.op("act", emit, reads=[in_] + list(extra_reads), writes=[out])

        def tt(eng, out, a, b, op):
            S.op(eng, lambda e: e.tensor_tensor(out=out.ap, in0=a.ap, in1=b.ap, op=op), reads=[a, b], writes=[out])

        def ts(eng, out, a, s1, s2, op0, op1=None, extra_reads=()):
            def emit(e):
                if op1 is None:
                    return e.tensor_scalar(out=out.ap, in0=a.ap, scalar1=s1, scalar2=None, op0=op0)
                return e.tensor_scalar(out=out.ap, in0=a.ap, scalar1=s1, scalar2=s2, op0=op0, op1=op1)
            S.op(eng, emit, reads=[a] + list(extra_reads), writes=[out])

        def stt(eng, out, a, s, b, op0, op1, extra_reads=()):
            S.op(eng, lambda e: e.scalar_tensor_tensor(out=out.ap, in0=a.ap, scalar=s, in1=b.ap, op0=op0, op1=op1),
                 reads=[a, b] + list(extra_reads), writes=[out])

        def copy(eng, out, a):
            S.op(eng, lambda e: e.tensor_copy(out=out.ap, in_=a.ap), reads=[a], writes=[out])

        def memset(eng, out, val):
            S.op(eng, lambda e: e.memset(out.ap, val), writes=[out])

        def W3(slot, kind):
            if kind == "A":
                return slot.full.rearrange("p (k c) -> p k c", c=512)
            return slot.full.rearrange("p (k c) -> p k c", c=128)

        def rstd_from(ps, scale, nt=T, out=None):
            out = rstd if out is None else out
            act(sd.r(0, nt), ps, AF.Ln, scale=scale, bias=eps_ref.ap, extra_reads=[eps_ref])
            act(out.r(0, nt), sd.r(0, nt), AF.Exp, scale=-0.5)

        def apply_norm(src, gcol, dst, nt=T, rs=None, engs=("dve",)):
            rs = rstd if rs is None else rs
            for c in range(8):
                eng = engs[c % len(engs)]
                d_, s_ = dst.r(c * nt, (c + 1) * nt), src.r(c * nt, (c + 1) * nt)
                if eng == "pool":
                    assert dst.dt == F32
                    tt("pool", d_, s_, rs.r(0, nt), ALU.mult)
                    ts("pool", d_, d_, vec.full[:, gcol + c:gcol + c + 1], None, ALU.mult, extra_reads=[vref])
                else:
                    stt(eng, d_, s_, vec.full[:, gcol + c:gcol + c + 1], rs.r(0, nt), ALU.mult, ALU.mult, extra_reads=[vref])

        tcnt = [0]

        def touch(func):
            tcnt[0] += 1
            c = 1 + tcnt[0] % 127
            S.op("act", lambda e, c=c: e.activation(out=dummy.full[:, c:c + 1], in_=dummy.full[:, 0:1], func=func),
                 reads=[dummy.r(0, 1)], writes=[])

        def stats_oneshot(src, nt):
            act(sq.r(0, 8 * nt), src.r(0, 8 * nt), AF.Square)
            ps = next_aux(nt)
            mm(ps, [(ones.full, sq.full[:, c * nt:(c + 1) * nt]) for c in range(8)], [ones.r(), sq.r(0, 8 * nt)])
            return ps

        def proj_group(slot, j, src, ps, nk=8, kind="A"):
            w3 = W3(slot, kind)
            if kind == "A":
                pairs = [(w3[:, k, j * 128:(j + 1) * 128], src.full[:, k * T:(k + 1) * T]) for k in range(nk)]
                mm(ps, pairs, [slot.r(), src.r(0, nk * T)])
            else:
                for q in range(4):
                    ks = range(q * 8, q * 8 + 8)

                    def emit(e, ks=ks):
                        ins = None
                        for k in ks:
                            ins = e.matmul(ps.ap, lhsT=w3[:, k, :], rhs=src.full[:, k * T:(k + 1) * T],
                                           start=(k == 0), stop=(k == nk - 1))
                        return ins
                    S.op("pe", emit, reads=[slot.r(), src.r(q * 8 * T, (q + 1) * 8 * T)], writes=[ps])

        def chunk_kouter(slot, src, nk=8, hook=None, hook_at=4):
            pss = [next_ps() for _ in range(4)]
            w3 = W3(slot, "A")
            for k in range(nk):
                def emit(e, k=k):
                    ins = None
                    for j in range(4):
                        ins = e.matmul(pss[j].ap, lhsT=w3[:, k, j * 128:(j + 1) * 128], rhs=src.full[:, k * T:(k + 1) * T],
                                       start=(k == 0), stop=(k == nk - 1))
                    return ins
                S.op("pe", emit, reads=[slot.r(), src.c(k)], writes=pss)
                if hook is not None and k == hook_at:
                    hook()
            return pss

        def resid_phase(cids, kind, src, h, gnext=None):
            st = next_aux()
            cnt = [0]
            pend = []

            def stat(m):
                first, last = cnt[0] == 0, cnt[0] == 7
                cnt[0] += 1
                mm1(st, ones.full, sq.full[:, m * T:(m + 1) * T], first, last, [ones.r(), sq.c(m)])

            m = 0
            for ci, cid in enumerate(cids):
                slot = ring_get(cid)
                ng = 4 if kind == "A" else 1
                pss = chunk_kouter(slot, src) if (kind == "A" and ci == 0) else None
                for j in range(ng):
                    if pss is None:
                        ps = next_ps()
                        proj_group(slot, j, src, ps, nk=(8 if kind == "A" else 32), kind=kind)
                    else:
                        ps = pss[j]
                    tt("dve", h.c(m), h.c(m), ps, ALU.add)
                    act(sq.c(m), h.c(m), AF.Square)
                    if gnext is not None:
                        act(xg.c(m), h.c(m), AF.Copy, scale=vec.full[:, gnext + m:gnext + m + 1], extra_reads=[vref])
                    pend.append(m)
                    m += 1
                    if len(pend) > (4 if kind == "A" else 2):
                        stat(pend.pop(0))

            def flush():
                for mm_ in pend:
                    stat(mm_)
                assert cnt[0] == 8
            return st, flush

        def boundary(st, flush, h, gcol, first_cid, need_xn=True):
            slot = ring_get(first_cid)
            pss = chunk_kouter(slot, xg, hook=flush, hook_at=4)
            rstd_from(st, 1.0 / D)
            if need_xn:
                apply_norm(h, gcol, xn)
            return pss

        S.op("sp", lambda e: e.dma_start(out=vec.full, in_=vecs), writes=[vec.r()], dma_sem=sem["misc"])
        S.op("sp", lambda e: e.dma_start(out=identb.full, in_=ident_d), writes=[identb.r()], dma_sem=sem["misc"])
        S.op("sp", lambda e: e.dma_start(out=memf.r3(0, 8).ap, in_=memT3), writes=[memf.r()], dma_sem=sem["misc"])
        S.op("sp", lambda e: e.dma_start(out=hbuf[0].r3(0, 8).ap, in_=xT3[:, :, 0:T]), writes=[hbuf[0].r()], dma_sem=sem["x0"])
        memset("dve", ones.r(), 1.0)
        for l in range(DEPTH):
            memset("dve", gluh[l].r(), 0.0)
            memset("dve", chh[l].r(), 0.0)
        def build_diag(l, mode, js=range(4)):
            for j in js:
                n = l * 4 + j
                stage = (dstage if mode == "act" else dstage1)[n % 2]
                wc = l * NV_L + V_CB + j * KB
                if mode == "act":
                    for k in range(KB):
                        S.op("act", lambda e, o=stage.full[:, k * 128:(k + 1) * 128], sc=vec.full[:, wc + k:wc + k + 1]:
                             e.activation(out=o, in_=identb.full, func=AF.Copy, scale=sc),
                             reads=[identb.r(), vref], writes=[stage.r(k * 128, (k + 1) * 128)])
                else:
                    st3 = stage.full[:, 0:KB * 128].rearrange("p (k q) -> p k q", q=128)
                    id3 = identb.full.unsqueeze(1).to_broadcast([128, KB, 128])
                    w3_ = vec.full[:, wc:wc + KB].unsqueeze(2).to_broadcast([128, KB, 128])
                    S.op("dve", lambda e, o=st3, a=id3, b=w3_: e.tensor_tensor(out=o, in0=a, in1=b, op=ALU.mult),
                         reads=[identb.r(), vref], writes=[stage.r()])
                cid = l * NCH_L + C_D + j
                S.op("sp", lambda e, o=wbf[cid], s=stage.full: e.dma_start(out=o, in_=s),
                     reads=[stage.r()], writes=[Ref(None, "wbf", cid, cid + 1)], dma_sem=next_wst())
                ring["seen"].add(cid)

        for stg in dstage + dstage1:
            memset("dve", stg.r(KB * 128, 4096), 0.0)
        def build_diag_piece(l, j, q):
            stage = dstage1[(l * 4 + j) % 2]
            wc = l * NV_L + V_CB + j * KB
            k0, k1 = q * 8, min(KB, q * 8 + 8)
            st3 = stage.full[:, k0 * 128:k1 * 128].rearrange("p (k q) -> p k q", q=128)
            id3 = identb.full.unsqueeze(1).to_broadcast([128, k1 - k0, 128])
            w3_ = vec.full[:, wc + k0:wc + k1].unsqueeze(2).to_broadcast([128, k1 - k0, 128])
            S.op("dve", lambda e, o=st3, a=id3, b=w3_: e.tensor_tensor(out=o, in0=a, in1=b, op=ALU.mult),
                 reads=[identb.r(), vref], writes=[stage.r(k0 * 128, k1 * 128)])
            if q == 3:
                cid = l * NCH_L + C_D + j
                S.op("sp", lambda e, o=wbf[cid], s=stage.full: e.dma_start(out=o, in_=s),
                     reads=[stage.r()], writes=[Ref(None, "wbf", cid, cid + 1)], dma_sem=next_wst())
                ring["seen"].add(cid)

        build_diag(0, "dve", [2, 3])
        stm = stats_oneshot(memf, NMEM)
        rstd_from(stm, 1.0 / D, NMEM)
        build_diag(0, "act", [0, 1])
        for l in range(nlayers):
            vb = l * NV_L
            for c in range(8):
                stt("dve", xn.r(c * NMEM, (c + 1) * NMEM), memf.c(c), vec.full[:, vb + V_MEM + c:vb + V_MEM + c + 1],
                    rstd.r(0, NMEM), ALU.mult, ALU.mult, extra_reads=[vref])
            for cc in range(2):
                slot = ring_get(l * NCH_L + C_KV + cc)
                w3 = W3(slot, "A")
                for j in range(4):
                    ps = next_ps(NMEM)
                    mm(ps, [(w3[:, k, j * 128:(j + 1) * 128], xn.full[:, k * NMEM:(k + 1) * NMEM]) for k in range(8)],
                       [slot.r(), xn.r(0, 8 * NMEM)])
                    copy("dve", KT[l].c(cc * 4 + j), ps)
            for cc in range(2):
                slot = ring_get(l * NCH_L + C_KV + 2 + cc)
                w3 = W3(slot, "A")
                for mc in range(2):
                    ps = next_ps()
                    mm(ps, [(xn.full[:, k * NMEM + mc * 128:k * NMEM + (mc + 1) * 128], w3[:, k, :]) for k in range(8)],
                       [slot.r(), xn.r(0, 8 * NMEM)])
                    copy("dve", VV[l].c(mc, cc * 512, (cc + 1) * 512), ps)

        memset("dve", dummy.r(), 1.0)
        final_pending = None

        def emit_final(fp):
            st_, flush_, h_, t0_ = fp
            flush_()
            rstd_from(st_, 1.0 / D)
            apply_norm(h_, V_FINAL, ostage)
            S.op("sp", lambda e, o=outT3[:, :, t0_:t0_ + T], s=ostage.r3(0, 8).ap: e.dma_start(out=o, in_=s),
                 reads=[ostage.r()], dma_sem=sem["ost"])

        for ti in range(ntiles):
            h = hbuf[ti % 2]
            t0 = ti * T
            if ti == 0:
                st0 = stats_oneshot(h, T)
                rstd_from(st0, 1.0 / D, out=rstd_n)
                apply_norm(h, V_MIX, xn, rs=rstd_n)
            pss_first = None
            for l in range(nlayers):
                vb = l * NV_L
                base = l * NCH_L
                last_layer = l == nlayers - 1
                if pss_first is None:
                    slot = ring_get(base + C_IN + 0)
                    hook = None
                    if final_pending is not None:
                        fp = final_pending
                        final_pending = None
                        hook = lambda fp=fp: emit_final(fp)
                    pss = chunk_kouter(slot, xn, hook=hook, hook_at=4)
                    scaled = False
                else:
                    pss = pss_first
                    scaled = True
                touch(AF.Sigmoid)
                if l == 0 and ti + 1 < ntiles:
                    nh = hbuf[(ti + 1) % 2]
                    S.op("sp", lambda e, o=nh.r3(0, 8).ap, s=xT3[:, :, t0 + T:t0 + 2 * T]: e.dma_start(out=o, in_=s),
                         writes=[nh.r()], dma_sem=sem[f"x{(ti + 1) % 2}"])
                copy("dve", glub.r3(0, 4, 0, HB), gluh[l].r3(0, 4))
                copy("dve", chb.r3(0, 4, 0, HA), chh[l].r3(0, 4))
                for j in range(4):
                    if scaled:
                        tt("dve", v_b.c(j), pss[j], rstd.r(), ALU.mult)
                    else:
                        act(v_b.c(j), pss[j], AF.Copy)
                if scaled:
                    apply_norm(h, vb + V_MIX, xn)
                slot = ring_get(base + C_IN + 2)
                for j in range(4):
                    ps = next_ps()
                    proj_group(slot, j, xg if scaled else xn, ps)
                    if scaled:
                        tt("dve", c_a.c(j), ps, rstd.r(), ALU.mult)
                    else:
                        act(c_a.c(j), ps, AF.Copy)
                slot = ring_get(base + C_IN + 1)
                for j in range(4):
                    ps = next_ps()
                    proj_group(slot, j, xn, ps)
                    act(sig.c(j), ps, AF.Sigmoid)
                    tt("dve", glub.c(j, HB, HB + T), v_b.c(j), sig.c(j), ALU.mult)
                touch(AF.Ln)
                copy("dve", gluh[l].r3(0, 4), glub.r3(0, 4, T, T + HB))
                wcb = vb + V_CB

                def partial(j):
                    if NDT == 0:
                        return
                    ts("dve", cb.c(j), glub.c(j, 0, T), vec.full[:, wcb + j * KB:wcb + j * KB + 1],
                       vec.full[:, vb + V_CBB + j:vb + V_CBB + j + 1], ALU.mult, ALU.add, extra_reads=[vref])
                    for k in range(1, NDT):
                        stt("dve", cb.c(j), glub.c(j, k, k + T), vec.full[:, wcb + j * KB + k:wcb + j * KB + k + 1],
                            cb.c(j), ALU.mult, ALU.add, extra_reads=[vref])

                wa = vb + V_CA

                def conv_a(j):
                    ts("dve", c_a.c(j), chb.c(j, 0, T), vec.full[:, wa + j:wa + j + 1], None, ALU.mult, extra_reads=[vref])
                    for k in range(1, KA):
                        stt("dve", c_a.c(j), chb.c(j, k, k + T), vec.full[:, wa + k * 4 + j:wa + k * 4 + j + 1],
                            c_a.c(j), ALU.mult, ALU.add, extra_reads=[vref])

                partial(0)
                slot = ring_get(base + C_IN + 3)
                for j in range(4):
                    ps = next_ps()
                    proj_group(slot, j, xn, ps)
                    tt("dve", chb.c(j, HA, HA + T), ps, c_a.c(j), ALU.mult)
                copy("dve", chh[l].r3(0, 4), chb.r3(0, 4, T, T + HA))
                partial(1)
                conv_a(0)
                partial(2)
                conv_a(1)
                partial(3)
                conv_a(2)
                conv_a(3)
                ps1 = next_aux()
                ps2 = next_aux()
                pend = []

                def ln_stat(j):
                    mm1(ps1, ones.full, sq.full[:, j * T:(j + 1) * T], j == 0, j == 3, [ones.r(), sq.c(j)])
                    mm1(ps2, ones.full, sq.full[:, (4 + j) * T:(5 + j) * T], j == 0, j == 3, [ones.r(), sq.c(4 + j)])

                for j in range(4):
                    slot = ring_get(base + C_D + j)
                    psc = next_ps()
                    g0 = j * (T + HB)
                    mm(psc, [(slot.full[:, k * 128:(k + 1) * 128], glub.full[:, g0 + k:g0 + k + T]) for k in range(NDT, KB)],
                       [slot.r(), glub.c(j, 0, T + HB)])
                    if NDT == 0:
                        act(cb.c(j), psc, AF.Identity, bias=vec.full[:, vb + V_CBB + j:vb + V_CBB + j + 1], extra_reads=[vref])
                    else:
                        tt("dve", cb.c(j), psc, cb.c(j), ALU.add)
                    copy("dve", sq.c(j), cb.c(j))
                    act(sq.c(4 + j), cb.c(j), AF.Square)
                    pend.append(j)
                    if NDT == 0 and len(pend) > 1:
                        ln_stat(pend.pop(0))
                slot = ring_get(base + C_IN + 4)
                psb = []
                for j in range(4):
                    ps = next_ps()
                    proj_group(slot, j, xn, ps)
                    psb.append(ps)
                    if NDT == 0:
                        if j == 1:
                            for jj in pend:
                                ln_stat(jj)
                    elif j < 2:
                        ln_stat(pend[2 * j])
                        ln_stat(pend[2 * j + 1])

                def ya(j):
                    tt("dve", ycat.c(j), psb[j], c_a.c(j), ALU.mult)

                def ln_a(j):
                    tt("dve", cb.c(j), cb.c(j), mean.r(), ALU.subtract)

                def ln_b(j):
                    tt("dve", cb.c(j), cb.c(j), rstd.r(), ALU.mult)
                    act(ycat.c(4 + j), cb.c(j), AF.Silu, scale=vec.full[:, vb + V_LNG + j:vb + V_LNG + j + 1],
                        bias=vec.full[:, vb + V_LNB + j:vb + V_LNB + j + 1], extra_reads=[vref])

                ya(0)
                ts("dve", mean.r(), ps1, 1.0 / 512, None, ALU.mult)
                tt("dve", msq.r(), mean.r(), mean.r(), ALU.mult)
                stt("dve", msq.r(), ps2, 1.0 / 512, msq.r(), ALU.mult, ALU.subtract)
                ts("dve", msq.r(), msq.r(), 0.0, None, ALU.max)
                rstd_from(msq.r(), 1.0)
                touch(AF.Silu)
                ln_a(0)
                ya(1)
                ln_a(1)
                ln_b(0)
                ya(2)
                ln_b(1)
                ln_a(2)
                ln_b(2)
                ya(3)
                ln_a(3)
                ln_b(3)
                touch(AF.Ln)
                st, flush = resid_phase([base + C_OUT, base + C_OUT + 1], "A", ycat, h, gnext=vb + V_X)

                pss = boundary(st, flush, h, vb + V_X, base + C_Q, need_xn=False)
                for j in range(4):
                    stt("dve", qT.c(j), pss[j], 1.0 / 16.0, rstd.r(), ALU.mult, ALU.mult)
                slot = ring_get(base + C_Q + 1)
                for j in range(4):
                    ps = next_ps()
                    proj_group(slot, j, xg, ps)
                    stt("dve", qT.c(4 + j), ps, 1.0 / 16.0, rstd.r(), ALU.mult, ALU.mult)

                def scores(hd):
                    for mc in range(2):
                        ps = next_ps()
                        mm(ps, [(KT[l].full[:, (hd * 2 + dk) * NMEM + mc * 128:(hd * 2 + dk) * NMEM + (mc + 1) * 128],
                                 qT.full[:, (hd * 2 + dk) * T:(hd * 2 + dk + 1) * T]) for dk in range(2)],
                           [KT[l].r(), qT.c(hd * 2), qT.c(hd * 2 + 1)])
                        act(PT[hd].c(mc), ps, AF.Exp)

                def sums_pv(hd):
                    pss_ = next_aux()
                    mm(pss_, [(ones.full, PT[hd].full[:, mc * T:(mc + 1) * T]) for mc in range(2)], [ones.r(), PT[hd].r()])
                    act(lnt[hd % 2].r(), pss_, AF.Ln)
                    act(rinv[hd].r(), lnt[hd % 2].r(), AF.Exp, scale=-1.0)
                    for dc in range(2):
                        ps = next_ps()
                        mm(ps, [(VV[l].full[:, mc * D + hd * 256 + dc * 128:mc * D + hd * 256 + (dc + 1) * 128],
                                 PT[hd].full[:, mc * T:(mc + 1) * T]) for mc in range(2)],
                           [VV[l].r(), PT[hd].r()])
                        tt("dve", oT.c(hd * 2 + dc), ps, rinv[hd].r(), ALU.mult)

                scores(0)
                scores(1)
                sums_pv(0)
                scores(2)
                sums_pv(1)
                scores(3)
                sums_pv(2)
                sums_pv(3)
                st, flush = resid_phase([base + C_XO, base + C_XO + 1], "A", oT, h, gnext=vb + V_FFN)

                pss = boundary(st, flush, h, vb + V_FFN, base + C_UP)
                pre = last_layer and ti + 1 < ntiles
                hn = hbuf[(ti + 1) % 2]
                stn = None
                gi = 0
                for cc in range(8):
                    slot = ring_get(base + C_UP + cc) if cc > 0 else None
                    for j in range(4):
                        rt = rtmp[gi % 4]
                        if cc == 0:
                            ps = pss[j]
                        else:
                            ps = next_ps()
                            proj_group(slot, j, xg if cc == 1 else xn, ps)
                        act(rt.r(), ps, AF.Relu)
                        if cc <= 1:
                            tt("dve", rt.r(), rt.r(), rstd.r(), ALU.mult)
                        tt("dve", hid.c(cc * 4 + j), rt.r(), rt.r(), ALU.mult)
                        if ti == 0 and l == 0 and nlayers > 1 and gi < 16:
                            build_diag_piece(1, gi // 4, gi % 4)
                        if pre:
                            if 8 <= gi < 16:
                                act(sq.c(gi - 8), hn.c(gi - 8), AF.Square)
                            if 10 <= gi < 18:
                                if stn is None:
                                    stn = next_aux()
                                c = gi - 10
                                mm1(stn, ones.full, sq.full[:, c * T:(c + 1) * T], c == 0, c == 7, [ones.r(), sq.c(c)])
                            if gi == 18:
                                rstd_from(stn, 1.0 / D, out=rstd_n)
                        gi += 1
                if pre:
                    apply_norm(hn, V_MIX, xn, rs=rstd_n)
                gnext = None if last_layer else (l + 1) * NV_L + V_MIX
                st, flush = resid_phase([base + C_DN + m for m in range(8)], "B", hid, h, gnext=gnext)
                if not last_layer:
                    pss_first = boundary(st, flush, h, gnext, (l + 1) * NCH_L + C_IN + 0, need_xn=False)
                else:
                    final_pending = (st, flush, h, t0)
        emit_final(final_pending)

        S.wait_all("sp", [sem["ost"]] + [sem[f"wst{i}"] for i in range(NWST)])

        with nc.Block() as block:
            @block.tensor
            def _(e):
                S.emit("pe", e)

            @block.scalar
            def _(e):
                S.emit("act", e)

            @block.vector
            def _(e):
                S.emit("dve", e)

            @block.gpsimd
            def _(e):
                S.emit("pool", e)

            @block.sync
            def _(e):
                S.emit("sp", e)
    return nc


def _kind_a(W, c0):
    return np.ascontiguousarray(W[:, c0:c0 + 512].reshape(8, 128, 512).transpose(1, 0, 2)).reshape(128, 4096)


def _kind_b(W, c0):
    return np.ascontiguousarray(W[:, c0:c0 + 128].reshape(32, 128, 128).transpose(1, 0, 2)).reshape(128, 4096)


def _pack(inp):
    wts = np.empty((DEPTH * NW_L, 128, 4096), np.float32)
    vecs = np.zeros((128, NV), np.float32)

    def fm(v):
        v = np.asarray(v, np.float32)
        return v.reshape(-1, 128).T

    for l in range(DEPTH):
        b = l * NW_L
        wkv = np.asarray(inp["w_kv"][l])
        for i in range(4):
            wts[b + C_KV + i] = _kind_a(wkv, i * 512)
        win = np.asarray(inp["w_in"][l])
        for i, c0 in enumerate((1536, 2048, 512, 1024, 0)):
            wts[b + C_IN + i] = _kind_a(win, c0)
        for name, cb_, n in (("w_out", C_OUT, 2), ("w_q", C_Q, 2), ("w_xo", C_XO, 2), ("w_up", C_UP, 8)):
            w = np.asarray(inp[name][l])
            for i in range(n):
                wts[b + cb_ + i] = _kind_a(w, i * 512)
        wd = np.asarray(inp["w_down"][l])
        for i in range(8):
            wts[b + C_DN + i] = _kind_b(wd, i * 128)
        v = l * NV_L
        vecs[:, v + V_MIX:v + V_MIX + 8] = fm(inp["norm_mix_g"][l])
        vecs[:, v + V_X:v + V_X + 8] = fm(inp["norm_x_g"][l])
        vecs[:, v + V_FFN:v + V_FFN + 8] = fm(inp["norm_ffn_g"][l])
        vecs[:, v + V_MEM:v + V_MEM + 8] = fm(inp["norm_mem_g"][l])
        for k in range(KA):
            vecs[:, v + V_CA + k * 4:v + V_CA + k * 4 + 4] = fm(inp["conv_a_w"][l][k])
        for k in range(KB):
            wk = fm(inp["conv_b_w"][l][k])
            for j in range(4):
                vecs[:, v + V_CB + j * KB + k] = wk[:, j]
        vecs[:, v + V_CBB:v + V_CBB + 4] = fm(inp["conv_b_bias"][l])
        vecs[:, v + V_LNG:v + V_LNG + 4] = fm(inp["ln_b_g"][l])
        vecs[:, v + V_LNB:v + V_LNB + 4] = fm(inp["ln_b_b"][l])
    vecs[:, V_FINAL:V_FINAL + 8] = fm(inp["final_g"])
    vecs[:, V_EPS] = EPS
    return wts, vecs


def run(inp, ntiles=SEQ // T, nlayers=DEPTH, cores=NB, trace=False):
    wts, vecs = _pack(inp)
    x = np.asarray(inp["x"], np.float32)
    mem = np.asarray(inp["mem"], np.float32)
    nc = build_nc(ntiles, nlayers)
    ident = np.eye(128, dtype=np.float32)
    in_maps = []
    for b in range(cores):
        in_maps.append({"xT": np.ascontiguousarray(x[b].T), "memT": np.ascontiguousarray(mem[b].T),
                        "wts": wts, "vecs": vecs, "ident": ident})
    res = run_bass_kernel_spmd(nc, in_maps, core_ids=list(range(cores)), trace=trace)
    out = np.stack([np.ascontiguousarray(res.results[b]["outT"].T) for b in range(cores)], axis=0)
    return out, res


def kernel(**inputs):
    out, _ = run(inputs)
    return out.astype(np.float32)
```
